# Optimizing a Trainium2 kernel written in Bass

```python
import math
import jax
import jax.numpy as jnp
from jax import lax
import numpy as np

D_MODEL = 2048
BATCH = 4
SEQ = 4096
DEPTH = 2

CTX_LEN = 256
GRID_W = 64
HEAD_DIM = 128
DA_WIDTH = 3 * D_MODEL // 8
NA_WIDTH = 3 * D_MODEL // 8
MLP_WIDTH = D_MODEL // 4
DA_HEADS = DA_WIDTH // HEAD_DIM
NA_HEADS = NA_WIDTH // HEAD_DIM
MLP_GROUPS = MLP_WIDTH // HEAD_DIM
DA_QK_DIM = HEAD_DIM // 2
NA_WIN_H = 8
NA_WIN_W = 16
CHUNK = 128
Q_BLOCK = 128
ROPE_BASE = 10000.0
EPS = 1e-6
IN_SIZES = (DA_WIDTH,) * 4 + (NA_WIDTH,) * 4 + (MLP_WIDTH,) * 3
IN_OFFSETS = tuple(sum(IN_SIZES[:i]) for i in range(len(IN_SIZES) + 1))
IN_WIDTH = IN_OFFSETS[-1]
MIX_WIDTH = DA_WIDTH + NA_WIDTH + MLP_WIDTH

kernel_name = "hybrid_diffattn_natten_gmlp_prefix_dit"


def rms_norm(x, g):
    xf = x.astype(jnp.float32)
    y = xf * lax.rsqrt(jnp.mean(xf * xf, axis=-1, keepdims=True) + EPS)
    return (y * g.astype(jnp.float32)).astype(x.dtype)


def layer_norm(x, g, b):
    xf = x.astype(jnp.float32)
    mu = jnp.mean(xf, axis=-1, keepdims=True)
    xc = xf - mu
    y = xc * lax.rsqrt(jnp.mean(xc * xc, axis=-1, keepdims=True) + EPS)
    return (y * g.astype(jnp.float32) + b.astype(jnp.float32)).astype(x.dtype)


def _rotate(xp, ang):
    h = xp.shape[-1] // 2
    x1, x2 = xp[..., :h], xp[..., h:]
    cos, sin = jnp.cos(ang), jnp.sin(ang)
    return jnp.concatenate([x1 * cos - x2 * sin, x2 * cos + x1 * sin], axis=-1)


def rope_2d(x, row, col):
    n = x.shape[-1] // 4
    freqs = ROPE_BASE ** (-jnp.arange(n, dtype=jnp.float32) / n)
    ar = (row.astype(jnp.float32)[:, None] * freqs)[:, None, None, :]
    ac = (col.astype(jnp.float32)[:, None] * freqs)[:, None, None, :]
    xr = _rotate(x[..., :2 * n].astype(jnp.float32), ar)
    xc = _rotate(x[..., 2 * n:].astype(jnp.float32), ac)
    return jnp.concatenate([xr, xc], axis=-1).astype(x.dtype)


def diff_attn(q, k, v, lam):
    s = jnp.einsum('bqhid,bkhid->bhiqk', q, k).astype(jnp.float32) * (q.shape[-1] ** -0.5)
    p = jax.nn.softmax(s, axis=-1)
    a = p[:, :, 0] - lam * p[:, :, 1]
    return jnp.einsum('bhqk,bkhd->bqhd', a.astype(v.dtype), v)


def softmax_attn(q, k, v):
    s = jnp.einsum('bqhd,bkhd->bhqk', q, k).astype(jnp.float32) * (q.shape[-1] ** -0.5)
    p = jax.nn.softmax(s, axis=-1)
    return jnp.einsum('bhqk,bkhd->bqhd', p.astype(v.dtype), v)


def neighbourhood_attn(q, k, v, kc, vc, rpb):
    B, S, H, d = q.shape
    rows = S // GRID_W
    kh = min(NA_WIN_H, rows)
    kw = NA_WIN_W
    qg = q.reshape(B, rows, GRID_W, H, d)
    kg = k.reshape(B, rows, GRID_W, H, d)
    vg = v.reshape(B, rows, GRID_W, H, d)
    cols = jnp.arange(GRID_W)
    cs = jnp.clip(cols - kw // 2, 0, GRID_W - kw)
    colidx = cs[:, None] + jnp.arange(kw)[None, :]
    rel_col = colidx - cols[:, None] + (NA_WIN_W - 1)
    scale = d ** -0.5

    def one_row(r):
        rs = jnp.clip(r - kh // 2, 0, rows - kh)
        qr = lax.dynamic_index_in_dim(qg, r, axis=1, keepdims=False)
        kr = lax.dynamic_slice_in_dim(kg, rs, kh, axis=1)
        vr = lax.dynamic_slice_in_dim(vg, rs, kh, axis=1)
        kwin = kr[:, :, colidx]
        vwin = vr[:, :, colidx]
        rel_row = rs + jnp.arange(kh) - r + (NA_WIN_H - 1)
        bias = rpb[:, rel_row[None, :, None], rel_col[:, None, :]]
        s_loc = (jnp.einsum('bchd,bicjhd->bhcij', qr, kwin).astype(jnp.float32) * scale
                 + bias.astype(jnp.float32)[None])
        s_loc = s_loc.reshape(B, H, GRID_W, kh * kw)
        s_ctx = jnp.einsum('bchd,bnhd->bhcn', qr, kc).astype(jnp.float32) * scale
        p = jax.nn.softmax(jnp.concatenate([s_loc, s_ctx], axis=-1), axis=-1).astype(v.dtype)
        p_loc = p[..., :kh * kw].reshape(B, H, GRID_W, kh, kw)
        p_ctx = p[..., kh * kw:]
        return (jnp.einsum('bhcij,bicjhd->bchd', p_loc, vwin)
                + jnp.einsum('bhcn,bnhd->bchd', p_ctx, vc))

    out = lax.map(one_row, jnp.arange(rows))
    return out.transpose(1, 0, 2, 3, 4).reshape(B, S, H, d)


def spatial_gating(u, v, ln_g, ln_b, w_s, b_s):
    B, N, G, c = u.shape
    vn = layer_norm(v, ln_g, ln_b).reshape(B, N // CHUNK, CHUNK, G, c)
    mixed = jnp.einsum('gpq,bnqgc->bnpgc', w_s, vn) + b_s.T[None, None, :, :, None]
    return u * mixed.reshape(B, N, G, c)


def _split_cols(p):
    return [p[..., a:b] for a, b in zip(IN_OFFSETS[:-1], IN_OFFSETS[1:])]


def _branch_out(o, z, g, factor):
    return (rms_norm(o, g) * factor).reshape(z.shape) * jax.nn.silu(z)


def _mixer(h_a, h_n, h_m, z_a, z_n, z_m, w_out, da_g, na_g, mlp_g, lam_init):
    ya = _branch_out(h_a, z_a, da_g, 1.0 - lam_init)
    yn = _branch_out(h_n, z_n, na_g, 1.0)
    ym = _branch_out(h_m, z_m, mlp_g, 1.0)
    return jnp.concatenate([ya, yn, ym], axis=-1) @ w_out


def _layer(x, xc, c, c_ctx, ada_w, ada_b, pre_g, post_g, w_in, w_out,
           lam_q1, lam_k1, lam_q2, lam_k2, da_g, na_rpb, na_g,
           mlp_ln_g, mlp_ln_b, mlp_ws, mlp_bs, mlp_g, layer_idx, ctx_out):
    B, S, _ = x.shape
    C = xc.shape[1]
    lam_init = 0.8 - 0.6 * math.exp(-0.3 * layer_idx)
    lam = (jnp.exp(jnp.sum(lam_q1.astype(jnp.float32) * lam_k1.astype(jnp.float32)))
           - jnp.exp(jnp.sum(lam_q2.astype(jnp.float32) * lam_k2.astype(jnp.float32)))
           + lam_init)

    shift, scale, gate = jnp.split(jax.nn.silu(c) @ ada_w + ada_b, 3, axis=-1)
    shift_c, scale_c, gate_c = jnp.split(jax.nn.silu(c_ctx) @ ada_w + ada_b, 3, axis=-1)
    h = rms_norm(x, pre_g) * (1.0 + scale[:, None, :]) + shift[:, None, :]
    hc = rms_norm(xc, pre_g) * (1.0 + scale_c) + shift_c

    aq, ak, av, az, nq, nk, nv, nz, mu, mv, mz = _split_cols(h @ w_in)
    if ctx_out:
        caq, cak, cav, caz, cnq, cnk, cnv, cnz, cmu, cmv, cmz = _split_cols(hc @ w_in)
    else:
        cak, cav = jnp.split(hc @ w_in[:, IN_OFFSETS[1]:IN_OFFSETS[3]], 2, axis=-1)
        cnk, cnv = jnp.split(hc @ w_in[:, IN_OFFSETS[5]:IN_OFFSETS[7]], 2, axis=-1)

    pos = jnp.arange(S)
    row, col = pos // GRID_W, pos % GRID_W

    dq = rope_2d(aq.reshape(B, S, DA_HEADS, 2, DA_QK_DIM), row, col)
    dk = rope_2d(ak.reshape(B, S, DA_HEADS, 2, DA_QK_DIM), row, col)
    dv = av.reshape(B, S, DA_HEADS, HEAD_DIM)
    dkc = cak.reshape(B, C, DA_HEADS, 2, DA_QK_DIM)
    dvc = cav.reshape(B, C, DA_HEADS, HEAD_DIM)
    k_all = jnp.concatenate([dk, dkc], axis=1)
    v_all = jnp.concatenate([dv, dvc], axis=1)
    qb = dq.reshape(B, S // Q_BLOCK, Q_BLOCK, DA_HEADS, 2, DA_QK_DIM).swapaxes(0, 1)
    oa = lax.map(lambda qi: diff_attn(qi, k_all, v_all, lam), qb)
    oa = oa.swapaxes(0, 1).reshape(B, S, DA_HEADS, HEAD_DIM)

    nkc = cnk.reshape(B, C, NA_HEADS, HEAD_DIM)
    nvc = cnv.reshape(B, C, NA_HEADS, HEAD_DIM)
    on = neighbourhood_attn(nq.reshape(B, S, NA_HEADS, HEAD_DIM), nk.reshape(B, S, NA_HEADS, HEAD_DIM),
                            nv.reshape(B, S, NA_HEADS, HEAD_DIM), nkc, nvc, na_rpb)

    om = spatial_gating(jax.nn.gelu(mu).reshape(B, S, MLP_GROUPS, HEAD_DIM),
                        jax.nn.gelu(mv).reshape(B, S, MLP_GROUPS, HEAD_DIM),
                        mlp_ln_g, mlp_ln_b, mlp_ws, mlp_bs)

    y = _mixer(oa, on, om, az, nz, mz, w_out, da_g, na_g, mlp_g, lam_init)
    x = x + gate[:, None, :] * rms_norm(y, post_g)

    if ctx_out:
        oca = diff_attn(caq.reshape(B, C, DA_HEADS, 2, DA_QK_DIM), dkc, dvc, lam)
        ocn = softmax_attn(cnq.reshape(B, C, NA_HEADS, HEAD_DIM), nkc, nvc)
        ocm = spatial_gating(jax.nn.gelu(cmu).reshape(B, C, MLP_GROUPS, HEAD_DIM),
                             jax.nn.gelu(cmv).reshape(B, C, MLP_GROUPS, HEAD_DIM),
                             mlp_ln_g, mlp_ln_b, mlp_ws, mlp_bs)
        yc = _mixer(oca, ocn, ocm, caz, cnz, cmz, w_out, da_g, na_g, mlp_g, lam_init)
        xc = xc + gate_c * rms_norm(yc, post_g)
    return x, xc


def setup_inputs(seed: int = 0) -> dict:
    key = jax.random.key(seed)
    ks = jax.random.split(key, 24)
    n = jax.random.normal
    f = jnp.float32
    L, D = DEPTH, D_MODEL
    return {
        "x": n(ks[0], (BATCH, SEQ, D), f),
        "c": n(ks[1], (BATCH, D), f),
        "ctx": n(ks[2], (BATCH, CTX_LEN, D), f),
        "c_ctx": n(ks[3], (D,), f),
        "ada_w": n(ks[4], (L, D, 3 * D), f) * (0.5 * D ** -0.5),
        "ada_b": n(ks[5], (L, 3 * D), f) * 0.01,
        "pre_g": 1.0 + 0.05 * n(ks[6], (L, D), f),
        "post_g": 1.0 + 0.05 * n(ks[7], (L, D), f),
        "w_in": n(ks[8], (L, D, IN_WIDTH), f) * D ** -0.5,
        "w_out": n(ks[9], (L, MIX_WIDTH, D), f) * MIX_WIDTH ** -0.5,
        "lam_q1": 0.1 * n(ks[10], (L, DA_QK_DIM), f),
        "lam_k1": 0.1 * n(ks[11], (L, DA_QK_DIM), f),
        "lam_q2": 0.1 * n(ks[12], (L, DA_QK_DIM), f),
        "lam_k2": 0.1 * n(ks[13], (L, DA_QK_DIM), f),
        "da_g": 1.0 + 0.05 * n(ks[14], (L, HEAD_DIM), f),
        "na_rpb": 0.1 * n(ks[15], (L, NA_HEADS, 2 * NA_WIN_H - 1, 2 * NA_WIN_W - 1), f),
        "na_g": 1.0 + 0.05 * n(ks[16], (L, NA_HEADS, HEAD_DIM), f),
        "mlp_ln_g": 1.0 + 0.05 * n(ks[17], (L, MLP_GROUPS, HEAD_DIM), f),
        "mlp_ln_b": 0.02 * n(ks[18], (L, MLP_GROUPS, HEAD_DIM), f),
        "mlp_ws": n(ks[19], (L, MLP_GROUPS, CHUNK, CHUNK), f) * CHUNK ** -0.5,
        "mlp_bs": 1.0 + 0.05 * n(ks[20], (L, MLP_GROUPS, CHUNK), f),
        "mlp_g": 1.0 + 0.05 * n(ks[21], (L, MLP_GROUPS, HEAD_DIM), f),
    }


def reference(x, c, ctx, c_ctx, ada_w, ada_b, pre_g, post_g, w_in, w_out,
              lam_q1, lam_k1, lam_q2, lam_k2, da_g, na_rpb, na_g,
              mlp_ln_g, mlp_ln_b, mlp_ws, mlp_bs, mlp_g):
    xc = ctx
    for l in range(DEPTH):
        x, xc = _layer(x, xc, c, c_ctx, ada_w[l], ada_b[l], pre_g[l], post_g[l], w_in[l], w_out[l],
                       lam_q1[l], lam_k1[l], lam_q2[l], lam_k2[l], da_g[l], na_rpb[l], na_g[l],
                       mlp_ln_g[l], mlp_ln_b[l], mlp_ws[l], mlp_bs[l], mlp_g[l],
                       l, l < DEPTH - 1)
    return x
```

```python
import math
from contextlib import ExitStack

import numpy as np
import ml_dtypes
import concourse.bass as bass
import concourse.mybir as mybir
from concourse.bass_utils import run_bass_kernel_spmd

F32 = mybir.dt.float32
BF16 = mybir.dt.bfloat16
AF = mybir.ActivationFunctionType
ALU = mybir.AluOpType
AX = mybir.AxisListType

D = 2048
SEQ = 4096
HALF = 2048
CTX = 256
NTOK = HALF + CTX
INW = 7680
EPS = 1e-6
NEG = -30000.0
ENGS = ["sp", "act", "pool", "dve", "pe"]


import os
_KSTOP = os.environ.get("KSTOP", "")


class _Stop(Exception):
    pass


def _stage(name):
    if _KSTOP and _KSTOP == name:
        raise _Stop()


class Buf:
    __slots__ = ("name", "w", "r", "excl")

    def __init__(self, name, excl=False):
        self.name = name
        self.w = None
        self.r = {}
        self.excl = excl


class Prog:
    def __init__(self, nc, es):
        self.nc = nc
        self.sem = {e: es.enter_context(nc.semaphore("s_" + e)) for e in ENGS}
        self.cnt = {e: 0 for e in ENGS}
        self.ops = {e: [] for e in ENGS}
        self.seen = {e: {} for e in ENGS}
        nq = {"sp": 8, "pool": 6, "act": 2}
        self.dsem = {q: [es.enter_context(nc.semaphore(f"d_{q}{i}")) for i in range(k)] for q, k in nq.items()}
        self.dn = {q: 0 for q in nq}

    def _need(self, reads, writes):
        need = {}

        def add(tok):
            if tok is None:
                return
            s, v = tok
            if need.get(s, 0) < v:
                need[s] = v

        for b in reads:
            add(b.w)
        for b in writes:
            add(b.w)
            for s, v in b.r.items():
                add((s, v))
        return need

    def _commit(self, tok, reads, writes):
        s, v = tok
        for b in reads:
            if b.r.get(s, 0) < v:
                b.r[s] = v
        for b in writes:
            b.w = tok
            b.r = {}

    def _waits(self, eng, need):
        waits = []
        seen = self.seen[eng]
        for s, v in need.items():
            if eng == "pe" and s is self.sem["pe"]:
                continue
            if seen.get(s, 0) >= v:
                continue
            seen[s] = v
            waits.append((s, v))
        return waits

    @staticmethod
    def _excl(reads, writes):
        ex = [b for b in reads if b.excl]
        if not ex:
            return reads, writes
        return [b for b in reads if not b.excl], list(writes) + [b for b in ex if b not in writes]

    def op(self, eng, fn, reads=(), writes=()):
        reads, writes = self._excl(reads, writes)
        need = self._need(reads, writes)
        waits = self._waits(eng, need)
        self.cnt[eng] += 1
        tok = (self.sem[eng], self.cnt[eng])
        self.ops[eng].append((fn, waits, tok[0], 1))
        self._commit(tok, reads, writes)

    def dma(self, q, out, in_, reads=(), writes=()):
        need = self._need(reads, writes)
        n = self.dn[q]
        k = len(self.dsem[q])
        s = self.dsem[q][n % k]
        val = 16 * (n // k + 1)
        self.dn[q] += 1
        if val > 16 and need.get(s, 0) < val - 16:
            need[s] = val - 16
        waits = self._waits(q, need)
        self.ops[q].append((lambda e, out=out, in_=in_: e.dma_start(out=out, in_=in_), waits, s, 16))
        self._commit((s, val), reads, writes)

    def fence(self):
        need = {}
        for e in ENGS:
            if self.cnt[e] > 0:
                need[self.sem[e]] = self.cnt[e]
        for q in self.dsem:
            n = self.dn[q]
            k = len(self.dsem[q])
            for i in range(k):
                cnt = (n - i + k - 1) // k if n > i else 0
                if cnt > 0:
                    need[self.dsem[q][i]] = 16 * cnt
        for e in ENGS:
            nd = dict(need)
            waits = []
            seen = self.seen[e]
            for s_, v in nd.items():
                if seen.get(s_, 0) >= v:
                    continue
                seen[s_] = v
                waits.append((s_, v))
            if waits:
                self.ops[e].append((None, waits, None, 0))

    def copy(self, eng, out, in_, R, W):
        if eng == "act":
            self.op("act", lambda e: e.activation(out=out, in_=in_, func=AF.Copy), R, W)
        else:
            self.op(eng, lambda e: e.tensor_copy(out=out, in_=in_), R, W)

    def act(self, out, in_, func, R, W, scale=1.0, bias=0.0, accum=None):
        def f(e):
            kw = {}
            if accum is not None:
                kw["accum_out"] = accum
            return e.activation(out=out, in_=in_, func=func, bias=bias, scale=scale, **kw)
        self.op("act", f, R, W)

    def ts(self, eng, out, in0, s1, s2, op0, op1, R, W):
        if s2 is None:
            self.op(eng, lambda e: e.tensor_scalar(out=out, in0=in0, scalar1=s1, scalar2=None, op0=op0), R, W)
        else:
            self.op(eng, lambda e: e.tensor_scalar(out=out, in0=in0, scalar1=s1, scalar2=s2, op0=op0, op1=op1), R, W)

    def tt(self, eng, out, in0, in1, op, R, W):
        self.op(eng, lambda e: e.tensor_tensor(out=out, in0=in0, in1=in1, op=op), R, W)

    def stt(self, eng, out, in0, scalar, in1, op0, op1, R, W):
        self.op(eng, lambda e: e.scalar_tensor_tensor(out=out, in0=in0, scalar=scalar, in1=in1, op0=op0, op1=op1), R, W)

    def mm(self, out, lhsT, rhs, start, stop, R, W, skip=False):
        if skip:
            self.op("pe", lambda e: e.matmul(out, lhsT=lhsT, rhs=rhs, start=start, stop=stop, skip_group_check=True), R, W)
        else:
            self.op("pe", lambda e: e.matmul(out, lhsT=lhsT, rhs=rhs, start=start, stop=stop), R, W)

    def tr(self, out, in_, ident, R, W):
        self.op("pe", lambda e: e.transpose(out=out, in_=in_, identity=ident), R, W)

    def rsum(self, out, in_, R, W):
        self.op("dve", lambda e: e.reduce_sum(out=out, in_=in_, axis=AX.X), R, W)

    def recip(self, out, in_, R, W):
        self.op("dve", lambda e: e.reciprocal(out=out, in_=in_), R, W)

    def memset(self, eng, ap, val, R, W):
        self.op(eng, lambda e: e.memset(ap, val), R, W)

    def emit(self):
        nc = self.nc
        final = []
        for q in self.dsem:
            n = self.dn[q]
            k = len(self.dsem[q])
            for i in range(k):
                cnt = (n - i + k - 1) // k if n > i else 0
                if cnt > 0:
                    final.append((self.dsem[q][i], 16 * cnt))
        with nc.Block() as block:
            for e, deco in (("sp", block.sync), ("act", block.scalar), ("pool", block.gpsimd),
                            ("dve", block.vector), ("pe", block.tensor)):
                ops = self.ops[e]

                def body(eng, ops=ops, last=(e == "sp")):
                    for fn, waits, s, inc in ops:
                        for ws, wv in waits:
                            eng.wait_ge(ws, wv)
                        if fn is not None:
                            fn(eng).then_inc(s, inc)
                    if last:
                        for ws, wv in final:
                            eng.wait_ge(ws, wv)
                        for en in ENGS:
                            if self.cnt[en] > 0:
                                eng.wait_ge(self.sem[en], self.cnt[en])
                deco(body)


class Arena:
    def __init__(self, ap, n):
        self.ap = ap
        self.n = n
        self.off = 0
        self.hi = 0

    def mark(self):
        return self.off

    def reset(self, m):
        self.off = m

    def _take(self, nelem):
        nelem = (nelem + 15) // 16 * 16
        o = self.off
        self.off += nelem
        self.hi = max(self.hi, self.off)
        assert self.off <= self.n, f"arena overflow {self.off} > {self.n}"
        return self.ap[:, o:o + nelem]

    def bf(self, *shape):
        n = int(np.prod(shape))
        v = self._take(n)[:, 0:n]
        if len(shape) == 2:
            return v.rearrange("p (a b) -> p a b", b=shape[1])
        if len(shape) == 3:
            return v.rearrange("p (a b c) -> p a b c", b=shape[1], c=shape[2])
        return v

    def f32(self, *shape):
        n = int(np.prod(shape))
        v = self._take(2 * n)[:, 0:2 * n].bitcast(F32)
        if len(shape) == 2:
            return v.rearrange("p (a b) -> p a b", b=shape[1])
        if len(shape) == 3:
            return v.rearrange("p (a b c) -> p a b c", b=shape[1], c=shape[2])
        return v


class Ring:
    def __init__(self, aps, name):
        self.aps = aps
        self.bufs = [Buf(f"{name}{i}") for i in range(len(aps))]
        self.i = 0

    def next(self):
        i = self.i % len(self.aps)
        self.i += 1
        return self.aps[i], self.bufs[i]


def build_program(layers, ctx_outs, fused=False):
    nc = bass.Bass("TRN2", target_bir_lowering=False)
    es = ExitStack()
    NL = len(layers)

    def din(name, shape, dt=F32):
        return nc.dram_tensor(name, list(shape), dt, kind="ExternalInput").ap()

    Dm = {}
    Dm["x_own"] = din("x_own", [HALF, D])
    Dm["x_oth"] = din("x_oth", [HALF, D])
    Dm["xc"] = din("xc", [CTX, D])
    Dm["cc"] = din("cc", [128, 32])
    Dm["ropeo"] = din("ropeo", [2, 128, HALF])
    Dm["ropex"] = din("ropex", [2, 128, HALF])
    Dm["perm"] = din("perm", [128, 128], BF16)
    Dm["ident"] = din("ident", [128, 128], BF16)
    LD = []
    for li in range(NL):
        d = {}
        d["ada_w"] = din(f"ada_w{li}", [D, 3 * D])
        d["adabc"] = din(f"adabc{li}", [128, 64])
        d["adabg"] = din(f"adabg{li}", [D])
        d["pre_gc"] = din(f"pre_gc{li}", [128, 16])
        d["post_g"] = din(f"post_g{li}", [D])
        d["w_in"] = din(f"w_in{li}", [D, INW])
        d["w_out"] = din(f"w_out{li}", [D, D])
        d["lamv"] = din(f"lamv{li}", [256])
        d["headg"] = din(f"headg{li}", [D])
        d["ln_gb"] = din(f"ln_gb{li}", [1024])
        d["wsT"] = din(f"wsT{li}", [128, 512])
        d["bsc"] = din(f"bsc{li}", [128, 4])
        d["nab"] = din(f"nab{li}", [4, 6, 1280, 512])
        LD.append(d)
    xo = nc.dram_tensor("xo", [HALF, D], F32, kind="ExternalOutput").ap()
    xco = None
    if (not fused) and ctx_outs[0]:
        xco = nc.dram_tensor("xco", [CTX, D], F32, kind="ExternalOutput").ap()
    kto = nc.dram_tensor("kto", [6, 128, HALF], BF16).ap()
    vo = nc.dram_tensor("vo", [6, HALF, 128], BF16).ap()
    nkh = nc.dram_tensor("nkh", [6, 128, 512], BF16).ap()
    nvh = nc.dram_tensor("nvh", [6, 512, 128], BF16).ap()
    catT = nc.dram_tensor("catT", [16, 128, NTOK], BF16).ap()
    gated = nc.dram_tensor("gated", [2, D], F32).ap()

    ARENA_N = 103 * 1024
    arena_t = es.enter_context(nc.sbuf_tensor("arena", [128, ARENA_N], BF16))
    ps = es.enter_context(nc.psum_tensor("ps", [128, 8, 512], F32))
    P = Prog(nc, es)
    A = Arena(arena_t, ARENA_N)
    pb = [Buf(f"bank{i}", excl=True) for i in range(8)]

    def areset(m):
        P.fence()
        A.reset(m)

    def psbf(b):
        return ps[:, b, :].bitcast(BF16)

    ident = A.bf(128)
    perm = A.bf(128)
    B_const = Buf("const")
    P.dma("sp", ident, Dm["ident"], [], [B_const])
    P.dma("sp", perm, Dm["perm"], [], [B_const])
    ccs = A.f32(32)
    P.dma("sp", ccs, Dm["cc"], [], [B_const])
    mods = A.f32(4, 16)
    B_mods = Buf("mods")
    neglam = A.f32(1)
    B_lam = Buf("lam")
    small = A.f32(64)
    B_small = Buf("small")
    ssall = A.f32(40)
    rstdall = A.f32(40)
    B_ss = Buf("ssall")
    B_rstd = Buf("rstdall")
    bsc = A.f32(4)
    B_mlpc = Buf("mlpconst")
    wsT = A.bf(4, 128)
    hT = A.bf(16, NTOK)
    B_hT = Buf("hT")
    wblk = [A.bf(16, 512) for _ in range(2)]
    wring = Ring(wblk, "wblk")
    PH = A.mark()

    def psum_rot(banks):
        st = {"i": 0}

        def nxt():
            b = banks[st["i"] % len(banks)]
            st["i"] += 1
            return b
        return nxt

    def emit_layer(li, l, ctx_out, x_own, x_oth, xc, out_x, out_xc):
        Ld = LD[li]
        lam_init = 0.8 - 0.6 * math.exp(-0.3 * l)
        areset(PH)
        m0 = A.mark()
        csb = A.bf(16, 2)
        tcc = A.f32(32)
        B_cs = Buf("cs")
        P.act(tcc, ccs, AF.Tanh, [B_const], [B_cs], scale=0.5)
        P.ts("dve", tcc, tcc, 0.5, 0.5, ALU.mult, ALU.add, [B_cs], [B_cs])
        P.tt("dve", csb.rearrange("p a b -> p (a b)"), tcc, ccs, ALU.mult, [B_cs, B_const], [B_cs])
        adabc = A.f32(64)
        adabg = A.f32(D)
        pregc = A.f32(16)
        gsb = A.f32(D)
        modc = A.f32(64)
        B_ad = Buf("adconst")
        B_gsb = Buf("gsb")
        P.dma("sp", adabc, Ld["adabc"], [], [B_ad])
        P.dma("sp", adabg[0:2, :], Ld["adabg"].partition_broadcast(2), [], [B_ad])
        P.dma("sp", pregc, Ld["pre_gc"], [], [B_ad])
        rot = psum_rot([0, 1, 2, 3])
        for nb in range(12):
            wb, Bw = wring.next()
            P.dma("pool", wb, Ld["ada_w"][:, nb * 512:(nb + 1) * 512].rearrange("(k p) n -> p k n", p=128), [], [Bw])
            if nb < 8:
                for i in range(4):
                    j = nb * 4 + i
                    for k in range(16):
                        P.mm(ps[:, 4, 2 * j:2 * j + 2], wb[:, k, i * 128:(i + 1) * 128], csb[:, k, :], k == 0, k == 15,
                             [Bw, B_cs], [pb[4]])
            else:
                b = rot()
                for k in range(16):
                    P.mm(ps[0:2, b, :], csb[:, k, :], wb[:, k, :], k == 0, k == 15, [Bw, B_cs], [pb[b]])
                c0 = (nb - 8) * 512
                P.tt("dve", gsb[0:2, c0:c0 + 512], ps[0:2, b, :], adabg[0:2, c0:c0 + 512], ALU.add, [pb[b], B_ad], [B_gsb])
        P.tt("dve", modc, ps[:, 4, 0:64], adabc, ALU.add, [pb[4], B_ad], [B_small])
        mv = modc.rearrange("p (j r) -> p j r", r=2)
        for r in range(2):
            P.stt("dve", mods[:, 2 * r, :], mv[:, 16:32, r], 1.0, pregc, ALU.add, ALU.mult, [B_small, B_ad], [B_mods])
            P.copy("dve", mods[:, 2 * r + 1, :], mv[:, 0:16, r], [B_small], [B_mods])
        B_gated = Buf("gated")
        P.dma("sp", gated, gsb[0:2, :], [B_gsb], [B_gated])
        lamb = A.f32(256)
        lp = A.f32(128)
        P.dma("sp", lamb, Ld["lamv"].partition_broadcast(128), [], [B_ad])
        P.tt("dve", lp[:, 0:64], lamb[:, 0:64], lamb[:, 64:128], ALU.mult, [B_ad], [B_small])
        P.tt("dve", lp[:, 64:128], lamb[:, 128:192], lamb[:, 192:256], ALU.mult, [B_ad], [B_small])
        P.rsum(small[:, 0:2], lp.rearrange("p (a b) -> p a b", b=64), [B_small], [B_small])
        P.act(small[:, 2:4], small[:, 0:2], AF.Exp, [B_small], [B_small])
        P.tt("dve", small[:, 4:5], small[:, 3:4], small[:, 2:3], ALU.subtract, [B_small], [B_small])
        P.ts("dve", neglam, small[:, 4:5], -lam_init, None, ALU.add, None, [B_small], [B_lam])
        areset(m0)
        _stage("adaln")

        rope = A.f32(2, HALF)
        B_rope = Buf("rope")
        headg = A.f32(D)
        B_hg = Buf("headg")
        lnrows = A.f32(2, 512)
        P.dma("sp", headg, Ld["headg"].partition_broadcast(128), [], [B_hg])
        P.ts("dve", headg[:, 0:768], headg[:, 0:768], 1.0 - lam_init, None, ALU.mult, None, [B_hg], [B_hg])
        P.dma("sp", lnrows.rearrange("p a b -> p (a b)"), Ld["ln_gb"].partition_broadcast(128), [], [B_mlpc])
        P.dma("pool", wsT.rearrange("p a b -> p (a b)"), Ld["wsT"], [], [B_mlpc])
        P.dma("sp", bsc, Ld["bsc"], [], [B_mlpc])
        PH2 = A.mark()

        def build_hT(tiles):
            m = A.mark()
            xs = Ring([A.f32(D) for _ in range(2)], "xs")
            xn = A.bf(4, D)
            B_xn = [Buf(f"xn{i}") for i in range(4)]
            junk = A.bf(D)
            B_junk = Buf("junk")
            nt = len(tiles)
            P.memset("dve", ssall, 0.0, [], [B_ss])
            for i, (src, mi) in enumerate(tiles):
                xt, Bx = xs.next()
                P.dma("sp", xt, src, [], [Bx])
                P.act(junk, xt, AF.Square, [Bx], [B_junk, B_ss], accum=ssall[:, i:i + 1])
            P.act(rstdall[:, 0:nt], ssall[:, 0:nt], AF.Ln, [B_ss], [B_rstd], scale=1.0 / D, bias=EPS)
            P.act(rstdall[:, 0:nt], rstdall[:, 0:nt], AF.Exp, [B_rstd], [B_rstd], scale=-0.5)
            rot = psum_rot([0, 1, 2, 3, 5, 6])
            nblk = (nt + 3) // 4
            ev = 0
            for blk in range(nblk):
                tl = tiles[blk * 4:(blk + 1) * 4]
                mi = tl[0][1]
                for j, (src, _) in enumerate(tl):
                    i = blk * 4 + j
                    xt, Bx = xs.next()
                    P.dma("sp", xt, src, [], [Bx])
                    P.ts("dve", xn[:, j, :], xt, rstdall[:, i:i + 1], None, ALU.mult, None, [Bx, B_rstd], [B_xn[j]])
                n = len(tl) * 128
                for k in range(16):
                    b = rot()
                    pv = psbf(b)
                    for j in range(len(tl)):
                        P.tr(pv[:, j * 128:(j + 1) * 128], xn[:, j, k * 128:(k + 1) * 128], ident, [B_xn[j], B_const], [pb[b]])
                    dst = hT[:, k, blk * 512:blk * 512 + n]
                    if ev % 2 == 0:
                        P.ts("dve", dst, pv[:, 0:n], mods[:, 2 * mi, k:k + 1], mods[:, 2 * mi + 1, k:k + 1], ALU.mult, ALU.add,
                             [pb[b], B_mods], [B_hT])
                    else:
                        P.act(dst, pv[:, 0:n], AF.Identity, [pb[b], B_mods], [B_hT],
                              scale=mods[:, 2 * mi, k:k + 1], bias=mods[:, 2 * mi + 1, k:k + 1])
                    ev += 1
            areset(m)

        def xtiles(src, n, mi):
            return [(src[i * 128:(i + 1) * 128, :], mi) for i in range(n)]

        def fm_proj(wc, c0, n, b, Bw):
            for k in range(16):
                P.mm(ps[:, b, 0:n], wc[:, k, :], hT[:, k, c0:c0 + n], k == 0, k == 15, [Bw, B_hT], [pb[b]])

        def tm_proj(w, c0, ncols, b, Bw):
            for k in range(16):
                P.mm(ps[:, b, 0:ncols], hT[:, k, c0:c0 + 128], w[:, k, :], k == 0, k == 15, [Bw, B_hT], [pb[b]])

        P.dma("sp", rope, Dm["ropex"].rearrange("a p n -> p a n"), [], [B_rope])
        build_hT(xtiles(x_oth, 16, 0))
        _stage("hTo")
        mO = A.mark()
        tmpb = Ring([A.bf(512) for _ in range(2)], "tmpb")
        tmpf = Ring([A.f32(512) for _ in range(4)], "tmpf")
        kst = A.bf(HALF)
        vst = A.bf(16, 128)
        B_kst = Buf("kst")
        B_vst = Buf("vst")
        rot = psum_rot([0, 1, 2, 3])
        B_kvo = [Buf(f"kvo{i}") for i in range(6)]

        def rope_apply(b, n, tc0, dst, Wd):
            xb, Bxb = tmpb.next()
            P.copy("act", xb[:, 0:n], ps[:, b, 0:n], [pb[b]], [Bxb])
            P.mm(ps[:, 7, 0:n], perm, xb[:, 0:n], True, True, [B_const, Bxb], [pb[7]])
            t1, B1 = tmpf.next()
            t2, B2 = tmpf.next()
            P.tt("dve", t1[:, 0:n], ps[:, b, 0:n], rope[:, 0, tc0:tc0 + n], ALU.mult, [pb[b], B_rope], [B1])
            P.tt("dve", t2[:, 0:n], ps[:, 7, 0:n], rope[:, 1, tc0:tc0 + n], ALU.mult, [pb[7], B_rope], [B2])
            P.tt("pool", dst, t1[:, 0:n], t2[:, 0:n], ALU.add, [B1, B2], Wd)

        for h in range(6):
            wb, Bw = wring.next()
            P.dma("pool", wb[:, :, 0:256], Ld["w_in"][:, h * 512 + 128:h * 512 + 384].rearrange("(k p) n -> p k n", p=128), [], [Bw])
            for c in range(4):
                b = rot()
                fm_proj(wb[:, :, 0:128], c * 512, 512, b, Bw)
                rope_apply(b, 512, c * 512, kst[:, c * 512:(c + 1) * 512], [B_kst])
            if h == 0:
                _stage("o_k0a")
            P.dma("sp", kto[h], kst, [B_kst], [B_kvo[h]])
            if h == 0:
                _stage("o_k0")
            for t in range(16):
                b = rot()
                tm_proj(wb[:, :, 128:256], t * 128, 128, b, Bw)
                P.copy("act", vst[:, t, :], ps[:, b, 0:128], [pb[b]], [B_vst])
            P.dma("sp", vo[h].rearrange("(t p) d -> p t d", p=128), vst, [B_vst], [B_kvo[h]])
            if h == 0:
                _stage("o_v0")
        B_halo = Buf("halodram")
        for h in range(6):
            wb, Bw = wring.next()
            c0 = 3072 + h * 512
            P.dma("pool", wb[:, :, 0:256], Ld["w_in"][:, c0 + 128:c0 + 384].rearrange("(k p) n -> p k n", p=128), [], [Bw])
            for part, tc0 in ((0, HALF - 256), (1, 0)):
                b = rot()
                fm_proj(wb[:, :, 0:128], tc0, 256, b, Bw)
                P.copy("act", kst[:, part * 256:(part + 1) * 256], ps[:, b, 0:256], [pb[b]], [B_kst])
            P.dma("sp", nkh[h], kst[:, 0:512], [B_kst], [B_halo])
            for i, t in enumerate((14, 15, 0, 1)):
                b = rot()
                tm_proj(wb[:, :, 128:256], t * 128, 128, b, Bw)
                P.copy("act", vst[:, i, :], ps[:, b, 0:128], [pb[b]], [B_vst])
            P.dma("sp", nvh[h].rearrange("(t p) d -> p t d", p=128), vst[:, 0:4, :], [B_vst], [B_halo])
        areset(mO)
        _stage("passO")

        P.dma("sp", rope, Dm["ropeo"].rearrange("a p n -> p a n"), [], [B_rope])
        build_hT(xtiles(x_own, 16, 0) + xtiles(xc, 2, 1))
        _stage("hTm")
        mM = A.mark()
        tmpb = Ring([A.bf(512) for _ in range(2)], "tmpb")
        tmpf = Ring([A.f32(512) for _ in range(4)], "tmpf")
        PT = Ring([A.bf(1024) for _ in range(2)], "PT")
        biasr = Ring([A.f32(512) for _ in range(3)], "bias")
        otile = Ring([A.f32(128) for _ in range(4)], "otile")
        ytile = Ring([A.f32(128) for _ in range(2)], "ytile")
        catt = Ring([A.bf(128) for _ in range(2)], "catt")
        catTs = Ring([A.bf(512) for _ in range(2)], "catTs")
        junk = A.bf(512)
        B_junk = Buf("junk2")
        recs = A.f32(9)
        nrec = A.f32(9)
        sst = A.f32(4)
        rstd4 = A.f32(4)
        B_fin = Buf("fin")
        mH = A.mark()
        NQT = 18 if ctx_out else 16

        def finalize(kind, nt, q0, ztile0, Zs, B_Z, hrow, nm):
            na = nm * nt
            nbk = (na + 2) // 3
            for bk in range(nbk):
                ns = min(3, na - bk * 3)
                src = ps[:, 4 + bk, 0:390].rearrange("p (s c) -> p s c", c=130)[:, 0:ns, 128]
                P.recip(recs[:, bk * 3:bk * 3 + ns], src, [pb[4 + bk]], [B_fin])
            if nm == 2:
                P.ts("dve", nrec[:, nt:2 * nt], recs[:, nt:2 * nt], neglam[:, 0:1], None, ALU.mult, None, [B_fin, B_lam], [B_fin])
            ots = []
            P.memset("dve", sst, 0.0, [], [B_fin])
            for t in range(nt):
                o, Bo = otile.next()
                a0 = t
                src0 = ps[:, 4 + a0 // 3, (a0 % 3) * 130:(a0 % 3) * 130 + 128]
                P.ts("dve", o, src0, recs[:, a0:a0 + 1], None, ALU.mult, None, [pb[4 + a0 // 3], B_fin], [Bo])
                if nm == 2:
                    a1 = nt + t
                    src1 = ps[:, 4 + a1 // 3, (a1 % 3) * 130:(a1 % 3) * 130 + 128]
                    P.stt("dve", o, src1, nrec[:, a1:a1 + 1], o, ALU.mult, ALU.add, [pb[4 + a1 // 3], B_fin, Bo], [Bo])
                P.act(junk[:, 0:128], o, AF.Square, [Bo], [B_junk, B_fin], accum=sst[:, t:t + 1])
                ots.append((o, Bo))
            P.act(rstd4[:, 0:nt], sst[:, 0:nt], AF.Ln, [B_fin], [B_fin], scale=1.0 / 128, bias=EPS)
            P.act(rstd4[:, 0:nt], rstd4[:, 0:nt], AF.Exp, [B_fin], [B_fin], scale=-0.5)
            pv = psbf(7)
            for t in range(nt):
                o, Bo = ots[t]
                y, By = ytile.next()
                P.stt("dve", y, o, rstd4[:, t:t + 1], headg[:, hrow * 128:(hrow + 1) * 128], ALU.mult, ALU.mult, [Bo, B_fin, B_hg], [By])
                ct, Bc = catt.next()
                P.tt("pool", ct, y, Zs[:, ztile0 + t, :], ALU.mult, [By, B_Z], [Bc])
                P.tr(pv[:, t * 128:(t + 1) * 128], ct, ident, [Bc, B_const], [pb[7]])
            cs_, Bcs = catTs.next()
            P.copy("act", cs_[:, 0:nt * 128], pv[:, 0:nt * 128], [pb[7]], [Bcs])
            P.dma("sp", catT[hrow][:, q0:q0 + nt * 128], cs_[:, 0:nt * 128], [Bcs], [B_cat])

        def attention(kind, QT, B_Q, KT, B_K, V, B_V, Zs, B_Z, hrow, chunks, hh):
            nm = 2 if kind == "da" else 1
            scale = 0.125 if kind == "da" else 128.0 ** -0.5
            for (q0, nq, ztile0, ktl) in chunks:
                nt = nq // 128
                na = nm * nt
                nbk = (na + 2) // 3
                P.memset("dve", ps[:, 4:4 + nbk, :], 0.0, [], [pb[4 + i] for i in range(nbk)])
                nk = len(ktl)

                def S(j):
                    kt, _ = ktl[j]
                    buf = j % 2
                    if nm == 2:
                        for m_ in range(2):
                            P.mm(ps[:, 2 * buf + m_, 0:nq], KT[64 * m_:64 * m_ + 64, kt * 128:(kt + 1) * 128],
                                 QT[64 * m_:64 * m_ + 64, q0:q0 + nq], True, True, [B_K, B_Q], [pb[2 * buf], pb[2 * buf + 1]])
                    else:
                        P.mm(ps[:, 2 * buf, 0:nq], KT[:, kt * 128:(kt + 1) * 128], QT[:, q0:q0 + nq], True, True,
                             [B_K, B_Q], [pb[2 * buf]])

                def E(j):
                    kt, bias = ktl[j]
                    buf = j % 2
                    pt, Bp = PT.next()
                    if nm == 2:
                        P.act(pt[:, 0:2 * nq].rearrange("p (m q) -> p m q", m=2), ps[:, 2 * buf:2 * buf + 2, 0:nq], AF.Exp,
                              [pb[2 * buf], pb[2 * buf + 1]], [Bp], scale=scale)
                    elif bias is None:
                        P.act(pt[:, 0:nq], ps[:, 2 * buf, 0:nq], AF.Exp, [pb[2 * buf]], [Bp], scale=scale)
                    else:
                        bt, Bb = biasr.next()
                        P.dma("sp", bt[:, 0:nq], bias, [], [Bb])
                        tf, Bt = tmpf.next()
                        P.stt("dve", tf[:, 0:nq], ps[:, 2 * buf, 0:nq], scale, bt[:, 0:nq], ALU.mult, ALU.add, [pb[2 * buf], Bb], [Bt])
                        P.act(pt[:, 0:nq], tf[:, 0:nq], AF.Exp, [Bt], [Bp])
                    return pt, Bp

                def PV(j, pt, Bp):
                    kt, _ = ktl[j]
                    for m_ in range(nm):
                        for t in range(nt):
                            a = m_ * nt + t
                            bk = 4 + a // 3
                            P.mm(ps[:, bk, (a % 3) * 130:(a % 3) * 130 + 129], pt[:, m_ * nq + t * 128:m_ * nq + (t + 1) * 128],
                                 V[:, kt, 0:129], False, False, [Bp, B_V], [pb[bk]], skip=True)

                S(0)
                for j in range(nk):
                    if j + 1 < nk:
                        S(j + 1)
                    pt, Bp = E(j)
                    PV(j, pt, Bp)
                finalize(kind, nt, q0, ztile0, Zs, B_Z, hrow, nm)

        B_cat = Buf("catTdram")
        rot = psum_rot([0, 1, 2, 3])

        def silu_from_psum(src, dst, Bsrc, Wd, n):
            tf, Bt = tmpf.next()
            P.act(tf[:, 0:n], src, AF.Tanh, Bsrc, [Bt], scale=0.5)
            P.ts("dve", tf[:, 0:n], tf[:, 0:n], 0.5, 0.5, ALU.mult, ALU.add, [Bt], [Bt])
            P.tt("dve", dst, tf[:, 0:n], src, ALU.mult, [Bt] + Bsrc, Wd)

        for h in range(6):
            areset(mH)
            KT = A.bf(4352)
            V = A.bf(34, 130)
            QT = A.bf(NTOK)
            Zs = A.bf(18, 128)
            B_K, B_V, B_Q, B_Z = Buf("KT"), Buf("V"), Buf("QT"), Buf("Zs")
            wb, Bw = wring.next()
            P.dma("pool", wb, Ld["w_in"][:, h * 512:(h + 1) * 512].rearrange("(k p) n -> p k n", p=128), [], [Bw])
            P.memset("pool", V[:, :, 128:130], 1.0, [], [B_V])
            P.dma("sp", KT[:, 2048:4096], kto[h], [B_kvo[h]], [B_K])
            P.dma("sp", V[:, 16:32, 0:128], vo[h].rearrange("(t p) d -> p t d", p=128), [B_kvo[h]], [B_V])
            for c in range(4):
                b = rot()
                fm_proj(wb[:, :, 0:128], c * 512, 512, b, Bw)
                rope_apply(b, 512, c * 512, QT[:, c * 512:(c + 1) * 512], [B_Q])
            if ctx_out:
                b = rot()
                fm_proj(wb[:, :, 0:128], 2048, 256, b, Bw)
                P.copy("act", QT[:, 2048:2304], ps[:, b, 0:256], [pb[b]], [B_Q])
            for c in range(4):
                b = rot()
                fm_proj(wb[:, :, 128:256], c * 512, 512, b, Bw)
                rope_apply(b, 512, c * 512, KT[:, c * 512:(c + 1) * 512], [B_K])
            b = rot()
            fm_proj(wb[:, :, 128:256], 2048, 256, b, Bw)
            P.copy("act", KT[:, 4096:4352], ps[:, b, 0:256], [pb[b]], [B_K])
            for t in range(18):
                b = rot()
                vt = t if t < 16 else 32 + (t - 16)
                if t < NQT:
                    tm_proj(wb[:, :, 256:512], t * 128, 256, b, Bw)
                    P.copy("act", V[:, vt, 0:128], ps[:, b, 0:128], [pb[b]], [B_V])
                    silu_from_psum(ps[:, b, 128:256], Zs[:, t, :], [pb[b]], [B_Z], 128)
                else:
                    tm_proj(wb[:, :, 256:384], t * 128, 128, b, Bw)
                    P.copy("act", V[:, vt, 0:128], ps[:, b, 0:128], [pb[b]], [B_V])
            allk = [(j, None) for j in range(34)]
            chunks = [(c * 512, 512, c * 4, allk) for c in range(4)]
            if ctx_out:
                chunks.append((2048, 256, 16, [(32, None), (33, None)]))
            if h == 0:
                _stage("da0p")
            attention("da", QT, B_Q, KT, B_K, V, B_V, Zs, B_Z, h, chunks, h)
            if h == 0:
                _stage("da0")
        _stage("da")

        NA_TILES = {0: list(range(0, 10)), 1: list(range(4, 12)), 2: list(range(8, 16)), 3: list(range(10, 20))}
        for h in range(6):
            areset(mH)
            KT = A.bf(2816)
            V = A.bf(22, 130)
            QT = A.bf(NTOK)
            Zs = A.bf(18, 128)
            B_K, B_V, B_Q, B_Z = Buf("KT"), Buf("V"), Buf("QT"), Buf("Zs")
            wb, Bw = wring.next()
            c0 = 3072 + h * 512
            P.dma("pool", wb, Ld["w_in"][:, c0:c0 + 512].rearrange("(k p) n -> p k n", p=128), [], [Bw])
            P.memset("pool", V[:, :, 128:130], 1.0, [], [B_V])
            P.dma("sp", KT[:, 0:256], nkh[h][:, 0:256], [B_halo], [B_K])
            P.dma("sp", KT[:, 2304:2560], nkh[h][:, 256:512], [B_halo], [B_K])
            nv = nvh[h].rearrange("(t p) d -> p t d", p=128)
            P.dma("sp", V[:, 0:2, 0:128], nv[:, 0:2, :], [B_halo], [B_V])
            P.dma("sp", V[:, 18:20, 0:128], nv[:, 2:4, :], [B_halo], [B_V])
            for c in range(5 if ctx_out else 4):
                b = rot()
                n = 512 if c < 4 else 256
                fm_proj(wb[:, :, 0:128], c * 512, n, b, Bw)
                P.copy("act", QT[:, c * 512:c * 512 + n], ps[:, b, 0:n], [pb[b]], [B_Q])
            for c in range(5):
                b = rot()
                n = 512 if c < 4 else 256
                fm_proj(wb[:, :, 128:256], c * 512, n, b, Bw)
                dst = KT[:, 256 + c * 512:256 + c * 512 + n] if c < 4 else KT[:, 2560:2816]
                P.copy("act", dst, ps[:, b, 0:n], [pb[b]], [B_K])
            for t in range(18):
                b = rot()
                vt = 2 + t if t < 16 else 20 + (t - 16)
                if t < NQT:
                    tm_proj(wb[:, :, 256:512], t * 128, 256, b, Bw)
                    P.copy("act", V[:, vt, 0:128], ps[:, b, 0:128], [pb[b]], [B_V])
                    silu_from_psum(ps[:, b, 128:256], Zs[:, t, :], [pb[b]], [B_Z], 128)
                else:
                    tm_proj(wb[:, :, 256:384], t * 128, 128, b, Bw)
                    P.copy("act", V[:, vt, 0:128], ps[:, b, 0:128], [pb[b]], [B_V])
            chunks = []
            for c in range(4):
                ktl = [(e, Ld["nab"][c, h, i * 128:(i + 1) * 128, :]) for i, e in enumerate(NA_TILES[c])]
                ktl += [(20, None), (21, None)]
                chunks.append((c * 512, 512, c * 4, ktl))
            if ctx_out:
                chunks.append((2048, 256, 16, [(20, None), (21, None)]))
            attention("na", QT, B_Q, KT, B_K, V, B_V, Zs, B_Z, 6 + h, chunks, h)
            if h == 0:
                _stage("na0")
        _stage("na")

        for g in range(4):
            areset(mH)
            gu = A.bf(18, 128)
            gv = A.f32(18, 128)
            Zs = A.bf(18, 128)
            B_gu, B_gv, B_Z = Buf("gu"), Buf("gv"), Buf("Zs")
            st1 = A.f32(18)
            st2 = A.f32(18)
            mean = A.f32(18)
            rs1 = A.f32(18)
            rs2 = A.f32(18)
            B_st = Buf("mlpstats")
            vnr = Ring([A.bf(128) for _ in range(2)], "vn")
            tA = Ring([A.f32(256) for _ in range(2)], "tA")
            tB = Ring([A.f32(256) for _ in range(2)], "tB")
            wb, Bw = wring.next()
            c0 = 6144 + g * 384
            P.dma("pool", wb[:, :, 0:384], Ld["w_in"][:, c0:c0 + 384].rearrange("(k p) n -> p k n", p=128), [], [Bw])
            P.memset("dve", st1, 0.0, [], [B_st])
            P.memset("dve", st2, 0.0, [], [B_st])
            for t in range(NQT):
                b = rot()
                tm_proj(wb[:, :, 0:384], t * 128, 384, b, Bw)
                a_, Ba = tA.next()
                b_, Bb = tB.next()
                uv = ps[:, b, 0:256]
                P.act(a_, uv, AF.Square, [pb[b]], [Ba])
                P.ts("dve", a_, a_, 0.044715, 1.0, ALU.mult, ALU.add, [Ba], [Ba])
                P.tt("dve", a_, a_, uv, ALU.mult, [Ba, pb[b]], [Ba])
                P.act(b_, a_, AF.Tanh, [Ba], [Bb], scale=0.7978845608028654)
                P.ts("dve", b_, b_, 0.5, 0.5, ALU.mult, ALU.add, [Bb], [Bb])
                P.tt("dve", gu[:, t, :], b_[:, 0:128], ps[:, b, 0:128], ALU.mult, [Bb, pb[b]], [B_gu])
                P.tt("dve", gv[:, t, :], b_[:, 128:256], ps[:, b, 128:256], ALU.mult, [Bb, pb[b]], [B_gv])
                silu_from_psum(ps[:, b, 256:384], Zs[:, t, :], [pb[b]], [B_Z], 128)
                P.rsum(st1[:, t:t + 1], gv[:, t, :], [B_gv], [B_st])
                P.act(junk[:, 0:128], gv[:, t, :], AF.Square, [B_gv], [B_junk, B_st], accum=st2[:, t:t + 1])
            n = NQT
            P.ts("dve", mean[:, 0:n], st1[:, 0:n], 1.0 / 128, None, ALU.mult, None, [B_st], [B_st])
            P.tt("dve", rs1[:, 0:n], mean[:, 0:n], mean[:, 0:n], ALU.mult, [B_st], [B_st])
            P.stt("dve", rs1[:, 0:n], st2[:, 0:n], 1.0 / 128, rs1[:, 0:n], ALU.mult, ALU.subtract, [B_st], [B_st])
            P.act(rs1[:, 0:n], rs1[:, 0:n], AF.Ln, [B_st], [B_st], bias=EPS)
            P.act(rs1[:, 0:n], rs1[:, 0:n], AF.Exp, [B_st], [B_st], scale=-0.5)
            P.memset("dve", st2, 0.0, [B_st], [B_st])
            for t in range(NQT):
                a_, Ba = tA.next()
                P.ts("dve", a_[:, 0:128], gv[:, t, :], mean[:, t:t + 1], rs1[:, t:t + 1], ALU.subtract, ALU.mult, [B_gv, B_st], [Ba])
                P.tt("pool", a_[:, 0:128], a_[:, 0:128], lnrows[:, 0, g * 128:(g + 1) * 128], ALU.mult, [Ba, B_mlpc], [Ba])
                vn, Bvn = vnr.next()
                P.tt("pool", vn, a_[:, 0:128], lnrows[:, 1, g * 128:(g + 1) * 128], ALU.add, [Ba, B_mlpc], [Bvn])
                b = rot()
                P.mm(ps[:, b, 0:128], wsT[:, g, :], vn, True, True, [B_mlpc, Bvn], [pb[b]])
                P.stt("dve", gv[:, t, :], ps[:, b, 0:128], bsc[:, g:g + 1], gu[:, t, :], ALU.add, ALU.mult, [pb[b], B_mlpc, B_gu], [B_gv])
                P.act(junk[:, 0:128], gv[:, t, :], AF.Square, [B_gv], [B_junk, B_st], accum=st2[:, t:t + 1])
            P.act(rs2[:, 0:n], st2[:, 0:n], AF.Ln, [B_st], [B_st], scale=1.0 / 128, bias=EPS)
            P.act(rs2[:, 0:n], rs2[:, 0:n], AF.Exp, [B_st], [B_st], scale=-0.5)
            hrow = 12 + g
            pv = psbf(7)
            for t4 in range(0, NQT, 4):
                nt = min(4, NQT - t4)
                for t in range(t4, t4 + nt):
                    y, By = ytile.next()
                    P.stt("dve", y, gv[:, t, :], rs2[:, t:t + 1], headg[:, hrow * 128:(hrow + 1) * 128], ALU.mult, ALU.mult, [B_gv, B_st, B_hg], [By])
                    ct, Bc = catt.next()
                    P.tt("pool", ct, y, Zs[:, t, :], ALU.mult, [By, B_Z], [Bc])
                    P.tr(pv[:, (t - t4) * 128:(t - t4 + 1) * 128], ct, ident, [Bc, B_const], [pb[7]])
                cs_, Bcs = catTs.next()
                P.copy("act", cs_[:, 0:nt * 128], pv[:, 0:nt * 128], [pb[7]], [Bcs])
                P.dma("sp", catT[hrow][:, t4 * 128:(t4 + nt) * 128], cs_[:, 0:nt * 128], [Bcs], [B_cat])

        _stage("mlp")
        areset(PH)
        ncols = NQT * 128
        for f in range(16):
            P.dma("sp", hT[:, f, 0:ncols], catT[f][:, 0:ncols], [B_cat], [B_hT])
        wo = [wblk[0], wblk[1], A.bf(16, 512), A.bf(16, 512)]
        B_wo = [wring.bufs[0], wring.bufs[1], Buf("wo2"), Buf("wo3")]
        for nb in range(4):
            P.dma("pool", wo[nb], Ld["w_out"][:, nb * 512:(nb + 1) * 512].rearrange("(k p) n -> p k n", p=128), [], [B_wo[nb]])
        Gl = A.f32(D)
        Gc = A.f32(D)
        pg = A.f32(D)
        B_G = Buf("G")
        P.dma("sp", Gl, gated[0].partition_broadcast(128), [B_gated], [B_G])
        P.dma("sp", pg, Ld["post_g"].partition_broadcast(128), [], [B_G])
        P.tt("pool", Gl, Gl, pg, ALU.mult, [B_G], [B_G])
        if ctx_out:
            P.dma("sp", Gc, gated[1].partition_broadcast(128), [B_gated], [B_G])
            P.tt("pool", Gc, Gc, pg, ALU.mult, [B_G], [B_G])
        xs = Ring([A.f32(D) for _ in range(2)], "xs2")
        yo = Ring([A.f32(D) for _ in range(2)], "yo")
        junk = A.bf(512)
        B_junk = Buf("junk3")
        for t in range(NQT):
            base = 4 * (t % 2)
            for nb in range(4):
                for f in range(16):
                    P.mm(ps[:, base + nb, :], hT[:, f, t * 128:(t + 1) * 128], wo[nb][:, f, :], f == 0, f == 15,
                         [B_hT, B_wo[nb]], [pb[base + nb]])
            sc = small[:, 8 + 8 * (t % 2):16 + 8 * (t % 2)]
            P.memset("dve", sc, 0.0, [B_small], [B_small])
            for nb in range(4):
                P.act(junk, ps[:, base + nb, :], AF.Square, [pb[base + nb]], [B_junk, B_small], accum=sc[:, nb:nb + 1])
            P.rsum(sc[:, 4:5], sc[:, 0:4], [B_small], [B_small])
            P.act(sc[:, 5:6], sc[:, 4:5], AF.Ln, [B_small], [B_small], scale=1.0 / D, bias=EPS)
            P.act(sc[:, 5:6], sc[:, 5:6], AF.Exp, [B_small], [B_small], scale=-0.5)
            xt, Bx = xs.next()
            src = x_own[t * 128:(t + 1) * 128, :] if t < 16 else xc[(t - 16) * 128:(t - 15) * 128, :]
            P.dma("sp", xt, src, [], [Bx])
            y, By = yo.next()
            G = Gl if t < 16 else Gc
            for nb in range(4):
                P.stt("dve", y[:, nb * 512:(nb + 1) * 512], ps[:, base + nb, :], sc[:, 5:6], G[:, nb * 512:(nb + 1) * 512],
                      ALU.mult, ALU.mult, [pb[base + nb], B_small, B_G], [By])
            P.tt("pool", y, y, xt, ALU.add, [By, Bx], [By])
            dst = out_x[t * 128:(t + 1) * 128, :] if t < 16 else out_xc[(t - 16) * 128:(t - 15) * 128, :]
            P.dma("sp", dst, y, [By], [])

    if not fused:
        try:
            emit_layer(0, layers[0], ctx_outs[0], Dm["x_own"], Dm["x_oth"], Dm["xc"], xo, xco)
        except _Stop:
            pass
    else:
        raise NotImplementedError
    P.emit()
    print("arena high-water (KiB/partition):", A.hi * 2 / 1024, "ops:", {e: len(P.ops[e]) for e in ENGS})
    return nc


def _rope_tables(half):
    pos = half * HALF + np.arange(HALF)
    row = (pos // 64).astype(np.float64)
    col = (pos % 64).astype(np.float64)
    freqs = 10000.0 ** (-np.arange(16, dtype=np.float64) / 16)
    C = np.zeros((128, HALF), np.float64)
    S = np.zeros((128, HALF), np.float64)
    for f in range(128):
        m = f % 64
        p = row if m < 32 else col
        mm_ = m % 32
        j = mm_ % 16
        ang = p * freqs[j]
        C[f] = np.cos(ang)
        S[f] = -np.sin(ang) if mm_ < 16 else np.sin(ang)
    return np.stack([C, S]).astype(np.float32)


def _perm():
    Pm = np.zeros((128, 128), np.float32)
    for f in range(128):
        mm_ = f % 32
        partner = f + 16 if mm_ < 16 else f - 16
        Pm[partner, f] = 1.0
    return Pm.astype(ml_dtypes.bfloat16)


_NA_TILES = {0: list(range(0, 10)), 1: list(range(4, 12)), 2: list(range(8, 16)), 3: list(range(10, 20))}


def _na_bias(rpb, half):
    out = np.full((4, 6, 1280, 512), NEG, np.float32)
    base = half * 32
    qc = np.arange(64)
    cs = np.clip(qc - 8, 0, 48)
    for c in range(4):
        et = _NA_TILES[c]
        for i, e in enumerate(et):
            for r2 in range(2):
                kr = base - 4 + 2 * e + r2
                if kr < 0 or kr > 63:
                    continue
                for qi in range(8):
                    qr = base + 8 * c + qi
                    rs = min(max(qr - 4, 0), 56)
                    if not (rs <= kr < rs + 8):
                        continue
                    rr = kr - qr + 7
                    kc = np.arange(64)
                    valid = (kc[:, None] >= cs[None, :]) & (kc[:, None] < cs[None, :] + 16)
                    rc = np.clip(kc[:, None] - qc[None, :] + 15, 0, 30)
                    blk = rpb[:, rr, :][:, rc]
                    blk = np.where(valid[None], blk, NEG).astype(np.float32)
                    out[c, :, i * 128 + r2 * 64:i * 128 + r2 * 64 + 64, qi * 64:(qi + 1) * 64] = blk
    return out


def _win_layout(w_in):
    cols = []
    for h in range(6):
        for j in range(4):
            cols.append(np.arange(j * 768 + h * 128, j * 768 + (h + 1) * 128))
    for h in range(6):
        for j in range(4):
            cols.append(np.arange(3072 + j * 768 + h * 128, 3072 + j * 768 + (h + 1) * 128))
    for g in range(4):
        for j in range(3):
            cols.append(np.arange(6144 + j * 512 + g * 128, 6144 + j * 512 + (g + 1) * 128))
    idx = np.concatenate(cols)
    return np.ascontiguousarray(w_in[:, idx])


def _layer_consts(inp, l):
    d = {}
    d["ada_w"] = np.ascontiguousarray(inp["ada_w"][l])
    ab = inp["ada_b"][l]
    d["adabc"] = np.ascontiguousarray(np.repeat(ab[:4096].reshape(32, 128).T, 2, axis=1))
    d["adabg"] = np.ascontiguousarray(ab[4096:])
    d["pre_gc"] = np.ascontiguousarray(inp["pre_g"][l].reshape(16, 128).T)
    d["post_g"] = np.ascontiguousarray(inp["post_g"][l])
    d["w_in"] = _win_layout(inp["w_in"][l])
    d["w_out"] = np.ascontiguousarray(inp["w_out"][l])
    d["lamv"] = np.concatenate([inp["lam_q1"][l], inp["lam_k1"][l], inp["lam_q2"][l], inp["lam_k2"][l]]).astype(np.float32)
    d["headg"] = np.concatenate([np.tile(inp["da_g"][l], 6), inp["na_g"][l].ravel(), inp["mlp_g"][l].ravel()]).astype(np.float32)
    d["ln_gb"] = np.concatenate([inp["mlp_ln_g"][l].ravel(), inp["mlp_ln_b"][l].ravel()]).astype(np.float32)
    d["wsT"] = np.ascontiguousarray(inp["mlp_ws"][l].transpose(2, 0, 1).reshape(128, 512))
    d["bsc"] = np.ascontiguousarray(inp["mlp_bs"][l].T)
    return d


_PROG_CACHE = {}


def _get_prog(key, *a, **k):
    if key not in _PROG_CACHE:
        _PROG_CACHE[key] = build_program(*a, **k)
    return _PROG_CACHE[key]


def _run_layer(inp, l, ctx_out, x_full, xc_full):
    nc = _get_prog(("layer", l, ctx_out), [l], [ctx_out], fused=False)
    lc = _layer_consts(inp, l)
    ropes = [_rope_tables(0), _rope_tables(1)]
    nabs = [_na_bias(inp["na_rpb"][l], 0), _na_bias(inp["na_rpb"][l], 1)]
    perm = _perm()
    ident = np.eye(128).astype(ml_dtypes.bfloat16)
    in_maps = []
    for core in range(8):
        b, half = core // 2, core % 2
        m = {}
        m["x_own"] = np.ascontiguousarray(x_full[b, half * HALF:(half + 1) * HALF])
        m["x_oth"] = np.ascontiguousarray(x_full[b, (1 - half) * HALF:(2 - half) * HALF])
        m["xc"] = np.ascontiguousarray(xc_full[b])
        vec = np.stack([inp["c"][b], inp["c_ctx"]], axis=-1)
        m["cc"] = np.ascontiguousarray(vec.reshape(16, 128, 2).transpose(1, 0, 2).reshape(128, 32)).astype(np.float32)
        m["ropeo"] = ropes[half]
        m["ropex"] = ropes[1 - half]
        m["perm"] = perm
        m["ident"] = ident
        for k_, v in lc.items():
            m[k_ + "0"] = v
        m["nab0"] = nabs[half]
        in_maps.append(m)
    res = run_bass_kernel_spmd(nc, in_maps, core_ids=list(range(8)))
    xn = np.empty_like(x_full)
    xcn = np.empty_like(xc_full) if ctx_out else None
    for core in range(8):
        b, half = core // 2, core % 2
        r = res.results[core]
        xn[b, half * HALF:(half + 1) * HALF] = r["xo"]
        if ctx_out and half == 0:
            xcn[b] = r["xco"]
    return xn, xcn


def kernel(**inputs):
    inp = {k: np.asarray(v) for k, v in inputs.items()}
    x = np.ascontiguousarray(inp["x"], dtype=np.float32)
    xc = np.ascontiguousarray(inp["ctx"], dtype=np.float32)
    x, xc = _run_layer(inp, 0, True, x, xc)
    x, _ = _run_layer(inp, 1, False, x, xc)
    return x.astype(np.float32)
```
